# Optimizing a Trainium2 kernel written in Bass

```python
import math
import jax
import jax.numpy as jnp
from jax import lax
import numpy as np

D_MODEL = 2048
BATCH = 2
SEQ = 4096
DEPTH = 4

GRID_W = 64
CTX_LEN = 256
N_MIXERS = 3
N_MOD = 9
NORM_EPS = 1e-6
ROPE_THETA = 10000.0
NEG_INF = -1e30
D_FF = 5504
A_HEAD_DIM = 128
A_HEADS = D_MODEL // A_HEAD_DIM
A_KV_HEADS = 4
A_GROUP = A_HEADS // A_KV_HEADS
A_Q_WIDTH = A_HEADS * A_HEAD_DIM
A_KV_WIDTH = A_KV_HEADS * A_HEAD_DIM
A_WINDOW = 128
A_BLOCK = 128
B_CHUNK = 128
B_WIDTH = 3 * D_MODEL
B_GROUPS = 8
B_GROUP_W = B_WIDTH // B_GROUPS
C_HEAD_DIM = 128
C_HEADS = D_MODEL // (2 * C_HEAD_DIM)
C_WIDTH = C_HEADS * 2 * C_HEAD_DIM
C_BLOCK = 128

N_A = len(range(0, DEPTH, N_MIXERS))
N_B = len(range(1, DEPTH, N_MIXERS))
N_C = len(range(2, DEPTH, N_MIXERS))

kernel_name = 'hybrid_diffusion_block'


def rms_norm(x, g, eps=NORM_EPS):
    xf = x.astype(jnp.float32)
    y = xf * lax.rsqrt(jnp.mean(xf * xf, axis=-1, keepdims=True) + eps)
    return (y * g.astype(jnp.float32)).astype(x.dtype)


def layer_norm(x, g, b, eps=NORM_EPS):
    xf = x.astype(jnp.float32)
    mu = jnp.mean(xf, axis=-1, keepdims=True)
    xc = xf - mu
    y = xc * lax.rsqrt(jnp.mean(xc * xc, axis=-1, keepdims=True) + eps)
    return (y * g.astype(jnp.float32) + b.astype(jnp.float32)).astype(x.dtype)


def axial_rope(n_tokens, head_dim):
    rows = n_tokens // GRID_W
    row = jnp.repeat(jnp.arange(rows, dtype=jnp.float32), GRID_W)
    col = jnp.tile(jnp.arange(GRID_W, dtype=jnp.float32), rows)
    axis_dim = head_dim // 2
    inv = ROPE_THETA ** (-jnp.arange(0, axis_dim, 2, dtype=jnp.float32) / axis_dim)
    ang = jnp.concatenate([row[:, None] * inv, col[:, None] * inv], axis=-1)
    return jnp.cos(ang), jnp.sin(ang)


def apply_rope(x, cos, sin):
    xr = x.reshape(*x.shape[:-1], -1, 2)
    x0, x1 = xr[..., 0], xr[..., 1]
    c = cos[:, None, :].astype(x.dtype)
    s = sin[:, None, :].astype(x.dtype)
    return jnp.stack([x0 * c - x1 * s, x0 * s + x1 * c], axis=-1).reshape(x.shape)


def swiglu(h, w_in, w_out):
    g, u = jnp.split(h @ w_in, 2, axis=-1)
    return (jax.nn.silu(g) * u) @ w_out


def _pre(t, g, m, s):
    return rms_norm(t, g[2 * s]) * (1 + m[:, :, 3 * s + 1]) + m[:, :, 3 * s]


def _post(y, g, m, s):
    return m[:, :, 3 * s + 2] * rms_norm(y, g[2 * s + 1])


def _band(t):
    b, s = t.shape[:2]
    nb = s // A_BLOCK
    tp = jnp.pad(t, ((0, 0), (A_BLOCK, A_BLOCK), (0, 0), (0, 0)))
    tp = tp.reshape(b, nb + 2, A_BLOCK, *t.shape[2:])
    return jnp.concatenate([tp[:, :-2], tp[:, 1:-1], tp[:, 2:]], axis=2)


def window_gqa(h, hc, w_in, w_out, sink, cos, sin, need_ctx):
    b, s, _ = h.shape
    L = hc.shape[1]
    nb = s // A_BLOCK
    scale = A_HEAD_DIM ** -0.5

    def proj(t):
        T = t.shape[1]
        q, k, v = jnp.split(t @ w_in, [A_Q_WIDTH, A_Q_WIDTH + A_KV_WIDTH], axis=-1)
        return (q.reshape(b, T, A_HEADS, A_HEAD_DIM),
                k.reshape(b, T, A_KV_HEADS, A_HEAD_DIM),
                v.reshape(b, T, A_KV_HEADS, A_HEAD_DIM))

    q, k, v = proj(h)
    qc, kc, vc = proj(hc)
    q = apply_rope(q, cos, sin)
    k = apply_rope(k, cos, sin)
    qb = q.reshape(b, nb, A_BLOCK, A_KV_HEADS, A_GROUP, A_HEAD_DIM)
    kb, vb = _band(k), _band(v)
    qpos = jnp.arange(nb)[:, None, None] * A_BLOCK + jnp.arange(A_BLOCK)[None, :, None]
    kpos = (jnp.arange(nb)[:, None, None] - 1) * A_BLOCK + jnp.arange(3 * A_BLOCK)[None, None, :]
    valid = (jnp.abs(kpos - qpos) <= A_WINDOW) & (kpos >= 0) & (kpos < s)
    s_band = jnp.einsum('bnqkgd,bnmkd->bnkgqm', qb, kb).astype(jnp.float32) * scale
    s_band = jnp.where(valid[None, :, None, None], s_band, NEG_INF)
    s_ctx = jnp.einsum('bnqkgd,blkd->bnkgql', qb, kc).astype(jnp.float32) * scale
    sink_l = jnp.broadcast_to(sink.reshape(A_KV_HEADS, A_GROUP, 1, 1).astype(jnp.float32), s_ctx.shape[:-1] + (1,))
    p = jax.nn.softmax(jnp.concatenate([sink_l, s_ctx, s_band], axis=-1), axis=-1).astype(v.dtype)
    o = (jnp.einsum('bnkgql,blkd->bnqkgd', p[..., 1:1 + L], vc)
         + jnp.einsum('bnkgqm,bnmkd->bnqkgd', p[..., 1 + L:], vb))
    y = o.reshape(b, s, A_Q_WIDTH) @ w_out
    yc = None
    if need_ctx:
        qg = qc.reshape(b, L, A_KV_HEADS, A_GROUP, A_HEAD_DIM)
        sc = jnp.einsum('blkgd,bmkd->bkglm', qg, kc).astype(jnp.float32) * scale
        sink_c = jnp.broadcast_to(sink.reshape(A_KV_HEADS, A_GROUP, 1, 1).astype(jnp.float32), sc.shape[:-1] + (1,))
        pc = jax.nn.softmax(jnp.concatenate([sink_c, sc], axis=-1), axis=-1)[..., 1:].astype(vc.dtype)
        oc = jnp.einsum('bkglm,bmkd->blkgd', pc, vc)
        yc = oc.reshape(b, L, A_Q_WIDTH) @ w_out
    return y, yc


def chunk_mlp(t, w_in, vn_g, vn_b, w_s, b_s, w_out):
    b, T, _ = t.shape
    u, v = jnp.split(jax.nn.gelu(t @ w_in), 2, axis=-1)
    v = layer_norm(v, vn_g, vn_b)
    v = v.reshape(b, T // B_CHUNK, B_CHUNK, B_GROUPS, B_GROUP_W)
    v = jnp.einsum('gpq,bnqgc->bnpgc', w_s, v) + b_s.T[:, :, None]
    return (u * v.reshape(b, T, B_WIDTH)) @ w_out


def diff_attn(h, hc, w_in, w_out, lq1, lk1, lq2, lk2, subln_g, lam_init, cos, sin, need_ctx):
    b, s, _ = h.shape
    L = hc.shape[1]
    nb = s // C_BLOCK
    scale = C_HEAD_DIM ** -0.5
    lam = (jnp.exp(jnp.sum(lq1.astype(jnp.float32) * lk1.astype(jnp.float32)))
           - jnp.exp(jnp.sum(lq2.astype(jnp.float32) * lk2.astype(jnp.float32))) + lam_init)

    def proj(t):
        T = t.shape[1]
        q, k, v = jnp.split(t @ w_in, 3, axis=-1)
        return (q.reshape(b, T, C_HEADS, 2, C_HEAD_DIM),
                k.reshape(b, T, C_HEADS, 2, C_HEAD_DIM),
                v.reshape(b, T, C_HEADS, 2 * C_HEAD_DIM))

    def rope2(t):
        return apply_rope(t.reshape(b, s, 2 * C_HEADS, C_HEAD_DIM), cos, sin).reshape(t.shape)

    def finish(o):
        o = rms_norm(o, subln_g) * (1 - lam_init)
        return o.reshape(o.shape[0], o.shape[1], C_WIDTH) @ w_out

    q, k, v = proj(h)
    qc, kc, vc = proj(hc)
    q, k = rope2(q), rope2(k)
    qblocks = jnp.moveaxis(q.reshape(b, nb, C_BLOCK, C_HEADS, 2, C_HEAD_DIM), 1, 0)

    def block(qb):
        s_ctx = jnp.einsum('bqhcd,bkhcd->bhcqk', qb, kc).astype(jnp.float32) * scale
        s_lat = jnp.einsum('bqhcd,bkhcd->bhcqk', qb, k).astype(jnp.float32) * scale
        p = jax.nn.softmax(jnp.concatenate([s_ctx, s_lat], axis=-1), axis=-1)
        a = (p[:, :, 0] - lam * p[:, :, 1]).astype(v.dtype)
        return (jnp.einsum('bhqk,bkhe->bqhe', a[..., :L], vc)
                + jnp.einsum('bhqk,bkhe->bqhe', a[..., L:], v))

    o = lax.map(block, qblocks)
    o = jnp.moveaxis(o, 0, 1).reshape(b, s, C_HEADS, 2 * C_HEAD_DIM)
    y = finish(o)
    yc = None
    if need_ctx:
        sc = jnp.einsum('bqhcd,bkhcd->bhcqk', qc, kc).astype(jnp.float32) * scale
        pc = jax.nn.softmax(sc, axis=-1)
        ac = (pc[:, :, 0] - lam * pc[:, :, 1]).astype(vc.dtype)
        yc = finish(jnp.einsum('bhqk,bkhe->bqhe', ac, vc))
    return y, yc


def setup_inputs(seed: int = 0) -> dict:
    key = jax.random.key(seed)
    ks = jax.random.split(key, 32)
    D = D_MODEL

    def nrm(k, shape, s):
        return jax.random.normal(k, shape, jnp.float32) * s

    return {
        'x': nrm(ks[0], (BATCH, SEQ, D), 1.0),
        'c': nrm(ks[1], (BATCH, D), 1.0),
        'ctx': nrm(ks[2], (BATCH, CTX_LEN, D), 1.0),
        'c_ctx': nrm(ks[3], (D,), 1.0),
        'ada_w': nrm(ks[4], (DEPTH, D, N_MOD * D), 0.5 * D ** -0.5),
        'ada_b': nrm(ks[5], (DEPTH, N_MOD * D), 0.01),
        'norm_g': 1.0 + nrm(ks[6], (DEPTH, 6, D), 0.02),
        'ffn_w_in': nrm(ks[7], (DEPTH, 2, D, 2 * D_FF), D ** -0.5),
        'ffn_w_out': nrm(ks[8], (DEPTH, 2, D_FF, D), D_FF ** -0.5),
        'a_w_in': nrm(ks[9], (N_A, D, A_Q_WIDTH + 2 * A_KV_WIDTH), D ** -0.5),
        'a_w_out': nrm(ks[10], (N_A, A_Q_WIDTH, D), A_Q_WIDTH ** -0.5),
        'a_sink': nrm(ks[11], (N_A, A_HEADS), 1.0),
        'b_w_in': nrm(ks[12], (N_B, D, 2 * B_WIDTH), D ** -0.5),
        'b_vnorm_g': 1.0 + nrm(ks[13], (N_B, B_WIDTH), 0.02),
        'b_vnorm_b': nrm(ks[14], (N_B, B_WIDTH), 0.02),
        'b_ws': nrm(ks[15], (N_B, B_GROUPS, B_CHUNK, B_CHUNK), B_CHUNK ** -0.5),
        'b_bs': 1.0 + nrm(ks[16], (N_B, B_GROUPS, B_CHUNK), 0.1),
        'b_w_out': nrm(ks[17], (N_B, B_WIDTH, D), B_WIDTH ** -0.5),
        'c_w_in': nrm(ks[18], (N_C, D, 3 * C_WIDTH), D ** -0.5),
        'c_w_out': nrm(ks[19], (N_C, C_WIDTH, D), C_WIDTH ** -0.5),
        'c_lq1': nrm(ks[20], (N_C, C_HEAD_DIM), 0.1),
        'c_lk1': nrm(ks[21], (N_C, C_HEAD_DIM), 0.1),
        'c_lq2': nrm(ks[22], (N_C, C_HEAD_DIM), 0.1),
        'c_lk2': nrm(ks[23], (N_C, C_HEAD_DIM), 0.1),
        'c_subln_g': 1.0 + nrm(ks[24], (N_C, 2 * C_HEAD_DIM), 0.02),
    }


def reference(x, c, ctx, c_ctx, ada_w, ada_b, norm_g, ffn_w_in, ffn_w_out,
              a_w_in, a_w_out, a_sink,
              b_w_in, b_vnorm_g, b_vnorm_b, b_ws, b_bs, b_w_out,
              c_w_in, c_w_out, c_lq1, c_lk1, c_lq2, c_lk2, c_subln_g):
    b, s, d = x.shape
    cos, sin = axial_rope(s, A_HEAD_DIM)
    for i in range(DEPTH):
        kind, j = i % N_MIXERS, i // N_MIXERS
        ctx_live = i < DEPTH - 1
        ctx_read = ctx_live or kind != 1
        g = norm_g[i]
        m = (jax.nn.silu(c) @ ada_w[i] + ada_b[i]).reshape(b, 1, N_MOD, d)
        x = x + 0.5 * _post(swiglu(_pre(x, g, m, 0), ffn_w_in[i, 0], ffn_w_out[i, 0]), g, m, 0)
        hc = None
        if ctx_read:
            mc = (jax.nn.silu(c_ctx) @ ada_w[i] + ada_b[i]).reshape(1, 1, N_MOD, d)
            ctx = ctx + 0.5 * _post(swiglu(_pre(ctx, g, mc, 0), ffn_w_in[i, 0], ffn_w_out[i, 0]), g, mc, 0)
            hc = _pre(ctx, g, mc, 1)
        h = _pre(x, g, m, 1)
        if kind == 0:
            y, yc = window_gqa(h, hc, a_w_in[j], a_w_out[j], a_sink[j], cos, sin, ctx_live)
        elif kind == 1:
            y = chunk_mlp(h, b_w_in[j], b_vnorm_g[j], b_vnorm_b[j], b_ws[j], b_bs[j], b_w_out[j])
            yc = chunk_mlp(hc, b_w_in[j], b_vnorm_g[j], b_vnorm_b[j], b_ws[j], b_bs[j], b_w_out[j]) if ctx_live else None
        else:
            lam_init = 0.8 - 0.6 * math.exp(-0.3 * i)
            y, yc = diff_attn(h, hc, c_w_in[j], c_w_out[j], c_lq1[j], c_lk1[j], c_lq2[j], c_lk2[j],
                              c_subln_g[j], lam_init, cos, sin, ctx_live)
        x = x + _post(y, g, m, 1)
        x = x + 0.5 * _post(swiglu(_pre(x, g, m, 2), ffn_w_in[i, 1], ffn_w_out[i, 1]), g, m, 2)
        if ctx_live:
            ctx = ctx + _post(yc, g, mc, 1)
            ctx = ctx + 0.5 * _post(swiglu(_pre(ctx, g, mc, 2), ffn_w_in[i, 1], ffn_w_out[i, 1]), g, mc, 2)
    return x
```

```python
import contextlib
import os
import math
import numpy as np
import ml_dtypes

import concourse.bass as bass
import concourse.mybir as mybir
from concourse.bass_utils import run_bass_kernel_spmd

F32 = mybir.dt.float32
BF16 = mybir.dt.bfloat16
AF = mybir.ActivationFunctionType
ALU = mybir.AluOpType

D = 2048
DC = D // 128
SEQ = 4096
NCORE = 8
NUSED = 2
TL = SEQ
TCX = 256
T = TL + TCX
NT = T // 128
SBT = 1152
DEPTH = 4
DFF = 5504
FC = DFF // 128
EPS = 1e-6
NMOD = 9
CTX = 256

SUPER = [list(range(0, 9)), list(range(9, 18)), list(range(18, 26)), list(range(26, 34))]


def sb_blocks(tiles):
    lat = [t for t in tiles if t < 32]
    cx = [t for t in tiles if t >= 32]
    out = []
    o = 0
    n = len(lat) * 128
    while n > 0:
        b = min(512, n)
        out.append((o, b, False))
        o += b
        n -= b
    if cx:
        out.append((o, len(cx) * 128, True))
    return out


class Buf:
    __slots__ = ("t", "name", "st")

    def __init__(self, t, name):
        self.t = t
        self.name = name
        self.st = {}

    def __getitem__(self, k):
        return self.t[k]


class Prog:
    def __init__(self):
        self.nc = bass.Bass("TRN2", target_bir_lowering=False)
        nc = self.nc
        self.es = contextlib.ExitStack()
        self.E = {"pe": nc.tensor, "act": nc.scalar, "dve": nc.vector, "pool": nc.gpsimd, "sp": nc.sync}
        self.sem = {}
        self.cnt = {}
        self.seen = {e: {} for e in self.E}
        for e in self.E:
            self.sem[("c", e)] = self.es.enter_context(nc.semaphore("c_" + e))
            self.cnt[e] = 0
        self.dq = {}
        for q, n in (("sp", 12), ("pool", 8)):
            sl = []
            for i in range(n):
                k = ("d", q, i)
                self.sem[k] = self.es.enter_context(nc.semaphore(f"d_{q}{i}"))
                sl.append([k, 0])
            self.dq[q] = [sl, 0]
        self.nbuf = 0
        self.fence = {}
        self.nops = 0
        self.limit = int(os.environ.get('KLIMIT', '0'))
        self.marks = []

    def sb(self, shape, dt, name=None, stack=None):
        self.nbuf += 1
        name = f"{name or 'sb'}_{self.nbuf}"
        t = (stack or self.es).enter_context(self.nc.sbuf_tensor(name, list(shape), dt))
        b = Buf(t, name)
        if self.fence:
            b.st[None] = [None, dict(self.fence)]
        if stack is not None:
            stack.callback(self._release, b)
        return b

    def _release(self, b):
        for st in b.st.values():
            if st[0]:
                sk, v = st[0]
                if self.fence.get(sk, 0) < v:
                    self.fence[sk] = v
            for sk, v in st[1].items():
                if self.fence.get(sk, 0) < v:
                    self.fence[sk] = v

    def ps(self, shape, dt, name=None, stack=None):
        self.nbuf += 1
        name = f"{name or 'ps'}_{self.nbuf}"
        t = (stack or self.es).enter_context(self.nc.psum_tensor(name, list(shape), dt))
        return Buf(t, name)

    def dram(self, name, shape, dt, kind=None):
        if kind:
            t = self.nc.dram_tensor(name, list(shape), dt, kind=kind)
        else:
            t = self.nc.dram_tensor(name, list(shape), dt)
        return Buf(t.ap(), name)

    @staticmethod
    def _norm(lst):
        out = []
        for x in lst:
            if isinstance(x, Buf):
                out.append((x, None))
            else:
                out.append(x)
        return out

    def _deps(self, reads, writes):
        deps = []
        for b, k in reads:
            keys = list(b.st.keys()) if k is None else (k, None)
            for kk in keys:
                st = b.st.get(kk)
                if st and st[0]:
                    deps.append(st[0])
        for b, k in writes:
            keys = list(b.st.keys()) if k is None else (k, None)
            for kk in keys:
                st = b.st.get(kk)
                if st:
                    if st[0]:
                        deps.append(st[0])
                    deps.extend(st[1].items())
        return deps

    def _update(self, reads, writes, tok):
        sk, v = tok
        for b, k in reads:
            st = b.st.setdefault(k, [None, {}])
            st[1][sk] = v
        for b, k in writes:
            if k is None:
                b.st.clear()
            b.st[k] = [tok, {}]

    def _wait(self, eng, deps):
        seen = self.seen[eng]
        own = ("c", eng)
        need = {}
        for sk, v in deps:
            if eng == "pe" and sk == own:
                continue
            if seen.get(sk, 0) < v and need.get(sk, 0) < v:
                need[sk] = v
        for sk, v in need.items():
            self.E[eng].wait_ge(self.sem[sk], v)
            seen[sk] = v

    def op(self, eng, fn, reads=(), writes=()):
        self.nops += 1
        if self.limit and self.nops > self.limit:
            return None
        reads = self._norm(reads)
        writes = self._norm(writes)
        self._wait(eng, self._deps(reads, writes))
        ins = fn(self.E[eng])
        if isinstance(ins, (list, tuple)):
            ins = ins[-1]
        self.cnt[eng] += 1
        ins.then_inc(self.sem[("c", eng)], 1)
        tok = (("c", eng), self.cnt[eng])
        self._update(reads, writes, tok)
        return tok

    def dma(self, q, out, in_, reads=(), writes=(), **kw):
        self.nops += 1
        if self.limit and self.nops > self.limit:
            return None
        reads = self._norm(reads)
        writes = self._norm(writes)
        sl, rr = self.dq[q]
        slot = sl[rr % len(sl)]
        self.dq[q][1] = rr + 1
        deps = self._deps(reads, writes)
        if slot[1] > 0:
            deps.append((slot[0], 16 * slot[1]))
        self._wait(q, deps)
        ins = self.E[q].dma_start(out=out, in_=in_, **kw)
        slot[1] += 1
        ins.then_inc(self.sem[slot[0]], 16)
        tok = (slot[0], 16 * slot[1])
        self._update(reads, writes, tok)
        return tok

    def cc(self, fn, reads=(), writes=()):
        reads = self._norm(reads)
        writes = self._norm(writes)
        k = ("cc",)
        if k not in self.sem:
            self.sem[k] = self.es.enter_context(self.nc.semaphore("ccsem"))
            self.cnt[k] = 0
        deps = self._deps(reads, writes)
        if self.cnt[k]:
            deps.append((k, self.cnt[k]))
        self._wait("pool", deps)
        ins = fn(self.E["pool"])
        self.cnt[k] += 1
        ins.then_inc(self.sem[k], 1)
        tok = (k, self.cnt[k])
        self._update(reads, writes, tok)
        return tok

    def finish(self):
        deps = []
        for q in self.dq:
            for sk, n in self.dq[q][0]:
                if n:
                    deps.append((sk, 16 * n))
        for e in self.E:
            if self.cnt[e]:
                deps.append((("c", e), self.cnt[e]))
        if ("cc",) in self.sem:
            deps.append((("cc",), self.cnt[("cc",)]))
        self._wait("sp", [d for d in deps if d[0] != ("c", "sp")])
        self.es.close()


class Builder:
    def __init__(self, stop_after=None):
        self.stop_after = stop_after
        self.p = Prog()
        p = self.p
        self.ext_specs = {}
        self.x_in = self.ext("x_in", [T, D], ("x",))
        self.cvec = self.ext("cvec", [2, D], ("cvec",))
        self.ada_b = self.ext("ada_b", [DEPTH, NMOD * D], ("full", "ada_b"))
        self.norm_g = self.ext("norm_g", [DEPTH, 6, D], ("full", "norm_g"))
        self.out = p.dram("out", [T, D], F32, kind="ExternalOutput")
        self.x_scr = p.dram("x_scr", [T, D], F32)
        self.y_scr = p.dram("y_scr", [T, D], F32)
        self.m_scr = p.dram("m_scr", [2, NMOD * D], F32)
        self.qT_scr = p.dram("qT_scr", [16, 128, T], BF16)
        self.kT_scr = p.dram("kT_scr", [16, 128, T], BF16)
        self.v_scr = p.dram("v_scr", [T, 6144], BF16)
        self.oT_scr = p.dram("oT_scr", [48, 128, T], BF16)
        self.wcache = {}
        self.ident = self.const_bf16("ident", [128, 128], lambda i: np.eye(128, dtype=np.float32))
        self.rmat = self.const_bf16("rmat", [128, 128], lambda i: _rmat())
        self.mlo = self.const_bf16("mlo", [128, 512], lambda i: np.tile(np.tril(np.ones((128, 128), np.float32)).T.T * 0 + (np.arange(128)[:, None] >= np.arange(128)[None, :]), (1, 4)).astype(np.float32))
        self.mhi = self.const_bf16("mhi", [128, 512], lambda i: np.tile((np.arange(128)[:, None] <= np.arange(128)[None, :]), (1, 4)).astype(np.float32))
        self.ones = p.sb([128, 128], BF16, "ones")
        p.op("dve", lambda e: e.memset(self.ones[:], 1.0), writes=[self.ones])
        self.epsb = p.sb([128, 1], F32, "epsb")
        p.op("dve", lambda e: e.memset(self.epsb[:], EPS), writes=[self.epsb])
        self.cos_in = self.ext("cosT", [128, SEQ], ("const", lambda i: _rope_tables()[0]))
        self.sin_in = self.ext("sinT", [128, SEQ], ("const", lambda i: _rope_tables()[1]))
        self.banks = [p.ps([128, 512], F32, f"bank{i}") for i in range(6)]
        self.bank_rr = 0
        self.tbanks = [p.ps([128, 8, 128], BF16, f"tbank{i}") for i in range(2)]
        self.t_rr = 0
        self.AT = p.sb([128, 2, 3, DC], F32, "AT")
        self.BT = p.sb([128, 2, 3, DC], F32, "BT")
        self.x_cur = self.x_in

    def ext(self, name, shape, spec):
        self.ext_specs[name] = spec
        return self.p.dram(name, shape, F32, kind="ExternalInput")

    def const_bf16(self, name, shape, fn):
        src = self.ext(name + "_in", shape, ("const", fn))
        b = self.p.sb(shape, BF16, name)
        self.p.dma("pool", b[:], src.t[:, :], reads=[src], writes=[b])
        return b

    def W(self, src, idx, K, N):
        key = (src,) + tuple(idx)
        if key not in self.wcache:
            nm = src + "_" + "_".join(str(i) for i in idx)
            self.wcache[key] = self.ext("w_" + nm, [K, N], ("mat", src, tuple(idx)))
        return self.wcache[key]

    def bank(self, lo=0, hi=6):
        b = self.banks[lo + self.bank_rr % (hi - lo)]
        self.bank_rr += 1
        return b

    def ada(self, layer):
        p = self.p
        adaw = self.W("ada_w", (layer,), D, NMOD * D)
        with contextlib.ExitStack() as st:
            cT = p.sb([128, 2, DC], F32, "cT", st)
            scT = p.sb([128, DC, 2], BF16, "scT", st)
            for r in range(2):
                p.dma("sp", cT[:, r, :], self.cvec.t[r].rearrange("(c q) -> q c", q=128), reads=[self.cvec], writes=[(cT, r)],
                      allow_slow_non_contiguous=True)
            p.op("act", lambda e: e.activation(out=scT[:].rearrange("q c r -> q r c"), in_=cT[:], func=AF.Silu),
                 reads=[cT], writes=[scT])
            slabs = [p.sb([128, DC, 512], BF16, "adaslab", st) for _ in range(3)]
            bsb = [p.sb([2, 512], F32, "adab", st) for _ in range(3)]
            msb = [p.sb([2, 512], F32, "adam", st) for _ in range(3)]
            for n in range(NMOD * D // 512):
                sl, bb, mm = slabs[n % 3], bsb[n % 3], msb[n % 3]
                p.dma("pool", sl[:], adaw.t[:, n * 512:(n + 1) * 512].rearrange("(c q) n -> q c n", q=128), reads=[adaw], writes=[sl])
                p.dma("sp", bb[:], self.ada_b.t[layer, n * 512:(n + 1) * 512].partition_broadcast(2), reads=[self.ada_b], writes=[bb])
                pb = self.bank()

                def mmf(e, sl=sl, pb=pb):
                    r_ = None
                    for k in range(DC):
                        r_ = e.matmul(pb[0:2, :], scT[:, k, :], sl[:, k, :], start=(k == 0), stop=(k == DC - 1))
                    return r_
                p.op("pe", mmf, reads=[scT, sl], writes=[pb])
                p.op("dve", lambda e, pb=pb, bb=bb, mm=mm: e.tensor_tensor(out=mm[:], in0=pb[0:2, :], in1=bb[:], op=ALU.add),
                     reads=[pb, bb], writes=[mm])
                p.dma("sp", self.m_scr.t[:, n * 512:(n + 1) * 512], mm[:], reads=[mm], writes=[self.m_scr])
            mT = p.sb([128, 2, NMOD, DC], F32, "mT", st)
            gT = p.sb([128, 6, DC], F32, "gT", st)
            for r in range(2):
                p.dma("sp", mT[:, r].rearrange("q j c -> q (j c)"), self.m_scr.t[r].rearrange("(jc q) -> q jc", q=128),
                      reads=[self.m_scr], writes=[(mT, r)], allow_slow_non_contiguous=True)
            p.dma("sp", gT[:].rearrange("q j c -> q (j c)"), self.norm_g.t[layer].rearrange("j (c q) -> q (j c)", q=128),
                  reads=[self.norm_g], writes=[gT], allow_slow_non_contiguous=True)
            for r in range(2):
                for s in range(3):
                    p.op("dve", lambda e, r=r, s=s: e.scalar_tensor_tensor(
                        out=self.AT[:, r, s, :], in0=mT[:, r, 3 * s + 1, :], scalar=1.0, in1=gT[:, 2 * s, :],
                        op0=ALU.add, op1=ALU.mult), reads=[(mT, r), gT], writes=[(self.AT, (r, s))])
                    p.op("dve", lambda e, r=r, s=s: e.tensor_copy(out=self.BT[:, r, s, :], in_=mT[:, r, 3 * s, :]),
                         reads=[(mT, r)], writes=[(self.BT, (r, s))])

    def rstd_of(self, sv, a, b, c):
        p = self.p
        p.op("act", lambda e: e.activation(out=sv[:, b:b + 1], in_=sv[:, a:a + 1], func=AF.Sqrt, scale=1.0 / D, bias=self.epsb[:, :]),
             reads=[(sv, a), self.epsb], writes=[(sv, b)])
        p.op("dve", lambda e: e.reciprocal(out=sv[:, c:c + 1], in_=sv[:, b:b + 1]), reads=[(sv, b)], writes=[(sv, c)])

    def post(self, layer, s, coef, final):
        p = self.p
        p.marks.append(("post", layer, s, p.nops))
        with contextlib.ExitStack() as st:
            Cb = [p.sb([128, D], F32, "Cb", st) for _ in range(2)]
            gp = p.sb([128, D], F32, "gpost", st)
            p.dma("sp", gp[:], self.norm_g.t[layer, 2 * s + 1, :].partition_broadcast(128), reads=[self.norm_g], writes=[gp])
            for r in range(2):
                p.dma("sp", Cb[r][:], self.m_scr.t[r, (3 * s + 2) * D:(3 * s + 3) * D].partition_broadcast(128),
                      reads=[self.m_scr], writes=[Cb[r]])
                p.op("dve", lambda e, r=r: e.tensor_tensor(out=Cb[r][:], in0=Cb[r][:], in1=gp[:], op=ALU.mult),
                     reads=[Cb[r], gp], writes=[Cb[r]])
            xt = [p.sb([128, D], F32, "xt", st) for _ in range(2)]
            yt = [p.sb([128, D], F32, "yt", st) for _ in range(2)]
            jk = [p.sb([128, D], BF16, "jk", st) for _ in range(2)]
            stat = [p.sb([128, 4], F32, "stat", st) for _ in range(2)]
            for ti in range(NT):
                r = 0 if ti < 32 else 1
                r0 = ti * 128
                x, y, sv, jj = xt[ti % 2], yt[ti % 2], stat[ti % 2], jk[ti % 2]
                p.dma("sp", x[:], self.x_cur.t[r0:r0 + 128, :], reads=[self.x_cur], writes=[x])
                p.dma("sp", y[:], self.y_scr.t[r0:r0 + 128, :], reads=[self.y_scr], writes=[y])
                p.op("dve", lambda e, y=y, sv=sv, jj=jj: e.scalar_tensor_tensor(
                    out=jj[:], in0=y[:], scalar=1.0, in1=y[:], op0=ALU.mult, op1=ALU.mult, accum_out=sv[:, 0:1]),
                    reads=[y], writes=[jj, (sv, 0)])
                self.rstd_of(sv, 0, 1, 2)
                if coef != 1.0:
                    p.op("dve", lambda e, sv=sv: e.tensor_scalar(out=sv[:, 2:3], in0=sv[:, 2:3], scalar1=coef, scalar2=None, op0=ALU.mult),
                         reads=[(sv, 2)], writes=[(sv, 2)])
                p.op("dve", lambda e, y=y, sv=sv, r=r: e.scalar_tensor_tensor(
                    out=y[:], in0=y[:], scalar=sv[:, 2:3], in1=Cb[r][:], op0=ALU.mult, op1=ALU.mult),
                    reads=[y, (sv, 2), Cb[r]], writes=[y])
                p.op("dve", lambda e, x=x, y=y: e.tensor_tensor(out=x[:], in0=x[:], in1=y[:], op=ALU.add), reads=[x, y], writes=[x])
                dst = self.out if final else self.x_scr
                p.dma("sp", dst.t[r0:r0 + 128, :], x[:], reads=[x], writes=[(dst, ti)])
        self.x_cur = self.x_scr

    def pre(self, s, tiles, hT):
        p = self.p
        p.marks.append(("pre", s, tiles[0], p.nops))
        with contextlib.ExitStack() as st:
            xt = [p.sb([128, D], F32, "xt", st) for _ in range(2)]
            xn = [p.sb([128, D], BF16, "xn", st) for _ in range(2)]
            stat = [p.sb([128, 4], F32, "stat", st) for _ in range(2)]
            for li, ti in enumerate(tiles):
                r = 0 if ti < 32 else 1
                r0 = ti * 128
                l0 = li * 128
                x, xb, sv = xt[li % 2], xn[li % 2], stat[li % 2]
                p.dma("sp", x[:], self.x_cur.t[r0:r0 + 128, :], reads=[self.x_cur], writes=[x])
                p.op("dve", lambda e, x=x, sv=sv, xb=xb: e.scalar_tensor_tensor(
                    out=xb[:], in0=x[:], scalar=1.0, in1=x[:], op0=ALU.mult, op1=ALU.mult, accum_out=sv[:, 0:1]),
                    reads=[x], writes=[xb, (sv, 0)])
                self.rstd_of(sv, 0, 1, 2)
                p.op("dve", lambda e, x=x, sv=sv, xb=xb: e.tensor_scalar(out=xb[:], in0=x[:], scalar1=sv[:, 2:3], scalar2=None, op0=ALU.mult),
                     reads=[x, (sv, 2)], writes=[xb])
                for g8 in range(2):
                    tb = self.tbanks[self.t_rr % 2]
                    self.t_rr += 1

                    def trf(e, xb=xb, g8=g8, tb=tb):
                        r_ = None
                        for j in range(8):
                            c = g8 * 8 + j
                            r_ = e.transpose(tb[:, j, :], xb[:, c * 128:(c + 1) * 128], self.ident[:, :])
                        return r_
                    p.op("pe", trf, reads=[xb, self.ident], writes=[tb])
                    for j in range(8):
                        c = g8 * 8 + j
                        p.op("act", lambda e, c=c, j=j, tb=tb, l0=l0, r=r: e.activation(
                            out=hT[:, c, l0:l0 + 128], in_=tb[:, j, :], func=AF.Identity,
                            scale=self.AT[:, r, s, c:c + 1], bias=self.BT[:, r, s, c:c + 1]),
                            reads=[tb, (self.AT, (r, s)), (self.BT, (r, s))], writes=[(hT, (c, li))])

    def proj_fm(self, XT, KC, Wb, cols, blocks, cb, st):
        p = self.p
        wb = [p.sb([128, KC, 128], BF16, "wfm", st) for _ in range(3)]
        for m, c0 in enumerate(cols):
            w = wb[m % 3]
            p.dma("pool", w[:], Wb.t[:, c0:c0 + 128].rearrange("(c q) f -> q c f", q=128), reads=[Wb], writes=[w])
            for bi, blk in enumerate(blocks):
                t0, nt, _ = blk
                pb = self.bank(3, 6)

                def mm(e, w=w, pb=pb, t0=t0, nt=nt):
                    r_ = None
                    for k in range(KC):
                        r_ = e.matmul(pb[:, 0:nt], w[:, k, :], XT[:, k, t0:t0 + nt], start=(k == 0), stop=(k == KC - 1))
                    return r_
                p.op("pe", mm, reads=[w, XT], writes=[pb])
                cb(m, bi, blk, pb)

    def proj_tm(self, XT, KC, Wb, col0, ncols, NB, ntiles, cb, st):
        p = self.p
        slabs = [p.sb([128, KC, NB], BF16, "wtm", st) for _ in range(2)]
        for nb in range(ncols // NB):
            sl = slabs[nb % 2]
            c0 = col0 + nb * NB
            p.dma("pool", sl[:], Wb.t[:, c0:c0 + NB].rearrange("(c q) n -> q c n", q=128), reads=[Wb], writes=[sl])
            for lt in range(ntiles):
                pb = self.bank(3, 6)

                def mm(e, sl=sl, pb=pb, lt=lt):
                    r_ = None
                    for k in range(KC):
                        r_ = e.matmul(pb[:, 0:NB], XT[:, k, lt * 128:(lt + 1) * 128], sl[:, k, :], start=(k == 0), stop=(k == KC - 1))
                    return r_
                p.op("pe", mm, reads=[sl, XT], writes=[pb])
                cb(lt, nb, pb)

    def y_store_cb(self, tiles, NB, st):
        p = self.p
        ys = [p.sb([128, NB], F32, "ys", st) for _ in range(4)]
        cnt = [0]

        def cb(lt, nb, pb):
            y_ = ys[cnt[0] % 4]
            cnt[0] += 1
            if cnt[0] % 2:
                p.op("act", lambda e: e.activation(out=y_[:], in_=pb[:, 0:NB], func=AF.Copy), reads=[pb], writes=[y_])
            else:
                p.op("dve", lambda e: e.tensor_copy(out=y_[:], in_=pb[:, 0:NB]), reads=[pb], writes=[y_])
            r0 = tiles[lt] * 128
            p.dma("sp", self.y_scr.t[r0:r0 + 128, nb * NB:(nb + 1) * NB], y_[:], reads=[y_], writes=[(self.y_scr, (tiles[lt], nb))])
        return cb

    def ffn(self, layer, which, s):
        p = self.p
        w_in = self.W("ffn_w_in", (layer, which), D, 2 * DFF)
        w_out = self.W("ffn_w_out", (layer, which), DFF, D)
        for tiles in SUPER:
            ntok = len(tiles) * 128
            blocks = sb_blocks(tiles)
            with contextlib.ExitStack() as st:
                actT = p.sb([128, FC, SBT], BF16, "actT", st)
                with contextlib.ExitStack() as st1:
                    hT = p.sb([128, DC, SBT], BF16, "hT", st1)
                    self.pre(s, tiles, hT)
                    wb = [p.sb([128, 2, DC, 128], BF16, "w1", st1) for _ in range(3)]
                    sg = [p.sb([128, 512], F32, "sg", st1) for _ in range(3)]
                    nsg = 0
                    for j in range(FC):
                        w = wb[j % 3]
                        for u_ in range(2):
                            c0 = u_ * DFF + j * 128
                            p.dma("pool", w[:, u_], w_in.t[:, c0:c0 + 128].rearrange("(c q) f -> q c f", q=128), reads=[w_in], writes=[(w, u_)])
                        for bi, (t0, nt, _) in enumerate(blocks):
                            pg = self.bank()
                            pu = self.bank()

                            def mm1(e, w=w, pg=pg, pu=pu, t0=t0, nt=nt):
                                r_ = None
                                for u_, pb in ((0, pg), (1, pu)):
                                    for k in range(DC):
                                        r_ = e.matmul(pb[:, 0:nt], w[:, u_, k, :], hT[:, k, t0:t0 + nt], start=(k == 0), stop=(k == DC - 1))
                                return r_
                            p.op("pe", mm1, reads=[w, hT], writes=[pg, pu])
                            s_ = sg[nsg % 3]
                            nsg += 1
                            p.op("act", lambda e, s_=s_, pg=pg, nt=nt: e.activation(out=s_[:, 0:nt], in_=pg[:, 0:nt], func=AF.Silu),
                                 reads=[pg], writes=[s_])
                            p.op("dve", lambda e, s_=s_, pu=pu, nt=nt, t0=t0, j=j: e.tensor_tensor(
                                out=actT[:, j, t0:t0 + nt], in0=s_[:, 0:nt], in1=pu[:, 0:nt], op=ALU.mult),
                                reads=[s_, pu], writes=[(actT, (j, bi))])
                p.marks.append(("ffn_phase2", tiles[0], p.nops))
                with contextlib.ExitStack() as st2:
                    self.proj_tm(actT, FC, w_out, 0, D, 256, len(tiles), self.y_store_cb(tiles, 256, st2), st2)
        self.post(layer, s, 0.5, False)

    def qkv_stage(self, hT, tiles, blocks, Wb, qcols, kcols, vcol0, vw, st):
        p = self.p
        g0 = tiles[0] * 128
        nlat = sum(nt for (_, nt, cx) in blocks if not cx)
        cs = p.sb([128, 2, SBT], F32, "cs", st)
        if nlat:
            p.dma("sp", cs[:, 0, 0:nlat], self.cos_in.t[:, g0:g0 + nlat], reads=[self.cos_in], writes=[(cs, 0)])
            p.dma("sp", cs[:, 1, 0:nlat], self.sin_in.t[:, g0:g0 + nlat], reads=[self.sin_in], writes=[(cs, 1)])
        qb = [p.sb([128, 512], BF16, "qb", st) for _ in range(2)]
        t1 = [p.sb([128, 512], F32, "t1", st) for _ in range(2)]
        t2 = [p.sb([128, 512], F32, "t2", st) for _ in range(2)]
        ob = [p.sb([128, 512], BF16, "ob", st) for _ in range(3)]
        cnt = [0]
        nq = len(qcols)

        def cb(m, bi, blk, pb):
            t0, nt, cx = blk
            i = cnt[0]
            cnt[0] += 1
            o_ = ob[i % 3]
            dst = self.qT_scr.t[m] if m < nq else self.kT_scr.t[m - nq]
            dkey = (self.qT_scr, (m, g0 + t0)) if m < nq else (self.kT_scr, (m - nq, g0 + t0))
            if cx:
                p.op("act", lambda e: e.activation(out=o_[:, 0:nt], in_=pb[:, 0:nt], func=AF.Copy), reads=[pb], writes=[o_])
            else:
                q_, a_, b_ = qb[i % 2], t1[i % 2], t2[i % 2]
                p.op("act", lambda e: e.activation(out=q_[:, 0:nt], in_=pb[:, 0:nt], func=AF.Copy), reads=[pb], writes=[q_])
                pr = self.bank(0, 3)
                p.op("pe", lambda e: e.matmul(pr[:, 0:nt], self.rmat[:, :], q_[:, 0:nt], start=True, stop=True), reads=[self.rmat, q_], writes=[pr])
                p.op("dve", lambda e: e.tensor_tensor(out=a_[:, 0:nt], in0=pb[:, 0:nt], in1=cs[:, 0, t0:t0 + nt], op=ALU.mult),
                     reads=[pb, (cs, 0), q_], writes=[a_])
                p.op("dve", lambda e: e.tensor_tensor(out=b_[:, 0:nt], in0=pr[:, 0:nt], in1=cs[:, 1, t0:t0 + nt], op=ALU.mult),
                     reads=[pr, (cs, 1)], writes=[b_])
                p.op("dve", lambda e: e.tensor_tensor(out=o_[:, 0:nt], in0=a_[:, 0:nt], in1=b_[:, 0:nt], op=ALU.add),
                     reads=[a_, b_], writes=[o_])
            p.dma("sp", dst[:, g0 + t0:g0 + t0 + nt], o_[:, 0:nt], reads=[o_], writes=[dkey])
        self.proj_fm(hT, DC, Wb, list(qcols) + list(kcols), blocks, cb, st)
        p.marks.append(("vproj", tiles[0], p.nops))
        vb = [p.sb([128, 512], BF16, "vb", st) for _ in range(3)]
        vc = [0]

        def cbv(lt, nb, pb):
            v_ = vb[vc[0] % 3]
            vc[0] += 1
            p.op("act", lambda e: e.activation(out=v_[:], in_=pb[:, 0:512], func=AF.Copy), reads=[pb], writes=[v_])
            r0 = tiles[lt] * 128
            p.dma("sp", self.v_scr.t[r0:r0 + 128, nb * 512:(nb + 1) * 512], v_[:], reads=[v_], writes=[(self.v_scr, (tiles[lt], nb))])
        self.proj_tm(hT, DC, Wb, vcol0, vw, 512, len(tiles), cbv, st)

    def out_stage(self, KC, Wb, NB):
        p = self.p
        for tiles in SUPER:
            ntok = len(tiles) * 128
            g0 = tiles[0] * 128
            with contextlib.ExitStack() as st:
                oT = p.sb([128, KC, SBT], BF16, "oT", st)
                for c in range(KC):
                    p.dma("sp", oT[:, c, 0:ntok], self.oT_scr.t[c, :, g0:g0 + ntok], reads=[self.oT_scr], writes=[(oT, c)])
                self.proj_tm(oT, KC, Wb, 0, D, NB, len(tiles), self.y_store_cb(tiles, NB, st), st)

    def mixer_A(self, layer, j):
        p = self.p
        Wi = self.W("a_w_in", (j,), D, 3072)
        Wo = self.W("a_w_out", (j,), D, D)
        sink = self.ext(f"a_sink_{j}", [128, 16], ("rowrep", "a_sink", j))
        for tiles in SUPER:
            with contextlib.ExitStack() as st:
                hT = p.sb([128, DC, SBT], BF16, "hT", st)
                self.pre(1, tiles, hT)
                self.qkv_stage(hT, tiles, sb_blocks(tiles), Wi, [h * 128 for h in range(16)], [2048 + k * 128 for k in range(4)],
                               2560, 512, st)
        scale = 128.0 ** -0.5
        p.marks.append(("attnA", p.nops))
        with contextlib.ExitStack() as st:
            esink = p.sb([128, 16], F32, "esink", st)
            p.dma("sp", esink[:], sink.t[:, :], reads=[sink], writes=[esink])
            p.op("act", lambda e: e.activation(out=esink[:], in_=esink[:], func=AF.Exp), reads=[esink], writes=[esink])
            pts = [p.sb([128, 512], BF16, "pt", st) for _ in range(3)]
            zs = [p.sb([128, 512], F32, "zs", st) for _ in range(2)]
            npt = 0
            for k in range(4):
                with contextlib.ExitStack() as sk:
                    kT = p.sb([128, T], BF16, "kT", sk)
                    V = p.sb([128, NT, 128], BF16, "V", sk)
                    q4 = p.sb([128, 4, T], BF16, "q4", sk)
                    o4 = p.sb([128, 4, T], BF16, "o4", sk)
                    p.dma("sp", kT[:], self.kT_scr.t[k], reads=[self.kT_scr], writes=[kT])
                    p.dma("sp", V[:], self.v_scr.t[:, k * 128:(k + 1) * 128].rearrange("(c q) d -> q c d", q=128), reads=[self.v_scr], writes=[V])
                    for g in range(4):
                        p.dma("sp", q4[:, g, :], self.qT_scr.t[4 * k + g], reads=[self.qT_scr], writes=[(q4, g)])
                    for n in range(NT):
                        if n < 32:
                            chunks = [(32, None), (33, None)] + ([(n - 1, self.mlo)] if n > 0 else []) + [(n, None)] + ([(n + 1, self.mhi)] if n < 31 else [])
                        else:
                            chunks = [(32, None), (33, None)]
                        po, pz = self.banks[0], self.banks[1]
                        for ci, (kc, mk) in enumerate(chunks):
                            ps_ = self.bank(2, 6)
                            p.op("pe", lambda e, ps_=ps_, kc=kc, n=n: e.matmul(
                                ps_[:, :].rearrange("q (g t) -> q g t", g=4), kT[:, kc * 128:(kc + 1) * 128], q4[:, :, n * 128:(n + 1) * 128],
                                start=True, stop=True), reads=[kT, q4], writes=[ps_])
                            pt = pts[npt % 3]
                            npt += 1
                            p.op("act", lambda e, ps_=ps_, pt=pt: e.activation(out=pt[:], in_=ps_[:, :], func=AF.Exp, scale=scale),
                                 reads=[ps_], writes=[pt])
                            if mk is not None:
                                p.op("dve", lambda e, pt=pt, mk=mk: e.tensor_tensor(out=pt[:], in0=pt[:], in1=mk[:], op=ALU.mult),
                                     reads=[pt, mk], writes=[pt])
                            first, last = ci == 0, ci == len(chunks) - 1

                            def pv(e, pt=pt, kc=kc, first=first, last=last, po=po, pz=pz):
                                e.matmul(po[:, :], V[:, kc, :], pt[:], start=first, stop=last)
                                return e.matmul(pz[:, :], self.ones[:, :], pt[:], start=first, stop=last)
                            p.op("pe", pv, reads=[V, pt, self.ones], writes=[po, pz])
                        z = zs[n % 2]
                        for g in range(4):
                            h = 4 * k + g
                            p.op("act", lambda e, z=z, g=g, h=h, pz=pz: e.activation(
                                out=z[:, g * 128:(g + 1) * 128], in_=pz[:, g * 128:(g + 1) * 128], func=AF.Identity, bias=esink[:, h:h + 1], scale=1.0),
                                reads=[pz, esink], writes=[(z, g)])
                        p.op("dve", lambda e, z=z: e.reciprocal(out=z[:], in_=z[:]), reads=[z], writes=[z])
                        p.op("dve", lambda e, z=z, po=po, n=n: e.tensor_tensor(
                            out=o4[:, :, n * 128:(n + 1) * 128], in0=po[:, :].rearrange("q (g t) -> q g t", g=4), in1=z[:].rearrange("q (g t) -> q g t", g=4), op=ALU.mult),
                            reads=[po, z], writes=[(o4, n)])
                    for g in range(4):
                        p.dma("sp", self.oT_scr.t[4 * k + g], o4[:, g, :], reads=[o4], writes=[(self.oT_scr, 4 * k + g)])
        self.out_stage(16, Wo, 512)
        self.post(layer, 1, 1.0, False)

    def mixer_C(self, layer, j):
        p = self.p
        lam_init = 0.8 - 0.6 * math.exp(-0.3 * layer)
        Wi = self.W("c_w_in", (j,), D, 3 * D)
        Wo = self.W("c_w_out", (j,), D, D)
        lvec = [self.ext(f"{nm}_{j}", [1, 128], ("row", nm, j)) for nm in ("c_lq1", "c_lk1", "c_lq2", "c_lk2")]
        subg = self.ext(f"c_subln_g_{j}", [1, 256], ("row", "c_subln_g", j))
        for tiles in SUPER:
            with contextlib.ExitStack() as st:
                hT = p.sb([128, DC, SBT], BF16, "hT", st)
                self.pre(1, tiles, hT)
                self.qkv_stage(hT, tiles, sb_blocks(tiles), Wi, [m * 128 for m in range(16)], [2048 + m * 128 for m in range(16)],
                               4096, 2048, st)
        scale = 128.0 ** -0.5
        with contextlib.ExitStack() as st:
            lv = p.sb([128, 4, 128], F32, "lv", st)
            for i in range(4):
                p.dma("sp", lv[:, i, :], lvec[i].t[0, :].partition_broadcast(128), reads=[lvec[i]], writes=[(lv, i)])
            lam = p.sb([128, 8], F32, "lam", st)
            ljk = p.sb([128, 128], F32, "ljk", st)
            for i in range(2):
                p.op("dve", lambda e, i=i: e.scalar_tensor_tensor(out=ljk[:], in0=lv[:, 2 * i, :], scalar=1.0, in1=lv[:, 2 * i + 1, :],
                                                                   op0=ALU.mult, op1=ALU.mult, accum_out=lam[:, i:i + 1]),
                     reads=[lv], writes=[ljk, (lam, i)])
            p.op("act", lambda e: e.activation(out=lam[:, 2:4], in_=lam[:, 0:2], func=AF.Exp), reads=[(lam, 0), (lam, 1)], writes=[(lam, 2)])
            p.op("dve", lambda e: e.tensor_tensor(out=lam[:, 4:5], in0=lam[:, 2:3], in1=lam[:, 3:4], op=ALU.subtract), reads=[(lam, 2)], writes=[(lam, 4)])
            p.op("dve", lambda e: e.tensor_scalar(out=lam[:, 5:6], in0=lam[:, 4:5], scalar1=lam_init, scalar2=-1.0, op0=ALU.add, op1=ALU.mult),
                 reads=[(lam, 4)], writes=[(lam, 5)])
            sg_ = p.sb([128, 2], F32, "subg", st)
            p.dma("sp", sg_[:], subg.t[0].rearrange("(h q) -> q h", q=128), reads=[subg], writes=[sg_], allow_slow_non_contiguous=True)
            p.op("dve", lambda e: e.tensor_scalar(out=sg_[:], in0=sg_[:], scalar1=1.0 - lam_init, scalar2=None, op0=ALU.mult), reads=[sg_], writes=[sg_])
            pts = [p.sb([128, 512], BF16, "pt", st) for _ in range(3)]
            npt = 0
            qblocks = [(i * 512, 512, False) for i in range(8)] + [(4096, 256, True)]
            for h in range(8):
                with contextlib.ExitStack() as sk:
                    kT = p.sb([128, 2, T], BF16, "kT", sk)
                    qT = p.sb([128, 2, T], BF16, "qT", sk)
                    V = p.sb([128, NT, 256], BF16, "V", sk)
                    o2 = p.sb([128, 2, T], BF16, "o2", sk)
                    On = [p.sb([128, 2, 512], F32, "On", sk) for _ in range(2)]
                    rz = p.sb([128, 512], F32, "rz", sk)
                    oc = p.sb([128, 2, 512], F32, "oc", sk)
                    sq = p.sb([128, 2, 512], BF16, "sq", sk)
                    rs = p.sb([128, 512], F32, "rs", sk)
                    for c in range(2):
                        p.dma("sp", kT[:, c, :], self.kT_scr.t[2 * h + c], reads=[self.kT_scr], writes=[(kT, c)])
                        p.dma("sp", qT[:, c, :], self.qT_scr.t[2 * h + c], reads=[self.qT_scr], writes=[(qT, c)])
                    p.dma("sp", V[:], self.v_scr.t[:, h * 256:(h + 1) * 256].rearrange("(c q) d -> q c d", q=128), reads=[self.v_scr], writes=[V])
                    for (q0, nq, cx) in qblocks:
                        chunks = [32, 33] if cx else list(range(34))
                        for c in range(2):
                            po0, po1, pz = self.banks[0], self.banks[1], self.banks[2]
                            for ci, kc in enumerate(chunks):
                                ps_ = self.bank(3, 5)
                                p.op("pe", lambda e, ps_=ps_, kc=kc, c=c, q0=q0, nq=nq: e.matmul(
                                    ps_[:, 0:nq], kT[:, c, kc * 128:(kc + 1) * 128], qT[:, c, q0:q0 + nq], start=True, stop=True),
                                    reads=[(kT, c), (qT, c)], writes=[ps_])
                                pt = pts[npt % 3]
                                npt += 1
                                p.op("act", lambda e, ps_=ps_, pt=pt, nq=nq: e.activation(out=pt[:, 0:nq], in_=ps_[:, 0:nq], func=AF.Exp, scale=scale),
                                     reads=[ps_], writes=[pt])
                                first, last = ci == 0, ci == len(chunks) - 1

                                def pv(e, pt=pt, kc=kc, first=first, last=last, nq=nq, po0=po0, po1=po1, pz=pz):
                                    e.matmul(po0[:, 0:nq], V[:, kc, 0:128], pt[:, 0:nq], start=first, stop=last)
                                    e.matmul(po1[:, 0:nq], V[:, kc, 128:256], pt[:, 0:nq], start=first, stop=last)
                                    return e.matmul(pz[:, 0:nq], self.ones[:, :], pt[:, 0:nq], start=first, stop=last)
                                p.op("pe", pv, reads=[V, pt, self.ones], writes=[po0, po1, pz])
                            p.op("dve", lambda e, nq=nq, pz=pz: e.reciprocal(out=rz[:, 0:nq], in_=pz[:, 0:nq]), reads=[pz], writes=[rz])
                            for hf, po in ((0, po0), (1, po1)):
                                p.op("dve", lambda e, hf=hf, po=po, c=c, nq=nq: e.tensor_tensor(
                                    out=On[c][:, hf, 0:nq], in0=po[:, 0:nq], in1=rz[:, 0:nq], op=ALU.mult), reads=[po, rz], writes=[(On[c], hf)])
                        p.op("dve", lambda e, nq=nq: e.scalar_tensor_tensor(out=oc[:, :, 0:nq], in0=On[1][:, :, 0:nq], scalar=lam[:, 5:6], in1=On[0][:, :, 0:nq],
                                                                             op0=ALU.mult, op1=ALU.add), reads=[On[0], On[1], (lam, 5)], writes=[oc])
                        p.op("dve", lambda e, nq=nq: e.tensor_tensor(out=sq[:, :, 0:nq], in0=oc[:, :, 0:nq], in1=oc[:, :, 0:nq], op=ALU.mult), reads=[oc], writes=[sq])
                        pss = self.banks[5]

                        def ssm(e, nq=nq, pss=pss):
                            e.matmul(pss[:, 0:nq], self.ones[:, :], sq[:, 0, 0:nq], start=True, stop=False)
                            return e.matmul(pss[:, 0:nq], self.ones[:, :], sq[:, 1, 0:nq], start=False, stop=True)
                        p.op("pe", ssm, reads=[sq, self.ones], writes=[pss])
                        p.op("act", lambda e, nq=nq, pss=pss: e.activation(out=rs[:, 0:nq], in_=pss[:, 0:nq], func=AF.Sqrt, scale=1.0 / 256, bias=self.epsb[:, :]),
                             reads=[pss, self.epsb], writes=[rs])
                        p.op("dve", lambda e, nq=nq: e.reciprocal(out=rs[:, 0:nq], in_=rs[:, 0:nq]), reads=[rs], writes=[rs])
                        for hf in range(2):
                            p.op("dve", lambda e, hf=hf, nq=nq, q0=q0: e.scalar_tensor_tensor(
                                out=o2[:, hf, q0:q0 + nq], in0=oc[:, hf, 0:nq], scalar=sg_[:, hf:hf + 1], in1=rs[:, 0:nq], op0=ALU.mult, op1=ALU.mult),
                                reads=[oc, sg_, rs], writes=[(o2, (hf, q0))])
                    for hf in range(2):
                        p.dma("sp", self.oT_scr.t[2 * h + hf], o2[:, hf, :], reads=[o2], writes=[(self.oT_scr, 2 * h + hf)])
        self.out_stage(16, Wo, 512)
        self.post(layer, 1, 1.0, False)

    def mixer_B(self, layer, j):
        p = self.p
        Wi = self.W("b_w_in", (j,), D, 12288)
        Wo = self.W("b_w_out", (j,), 6144, D)
        ws_in = self.ext(f"b_ws_{j}", [8, 128, 128], ("row", "b_ws", j))
        bs_in = self.ext(f"b_bs_{j}", [1, 1024], ("rowflat", "b_bs", j))
        vg_in = self.ext(f"b_vnorm_g_{j}", [1, 6144], ("row", "b_vnorm_g", j))
        vb_in = self.ext(f"b_vnorm_b_{j}", [1, 6144], ("row", "b_vnorm_b", j))
        with contextlib.ExitStack() as sc:
            wsn = p.sb([128, 8, 128], BF16, "wsn", sc)
            wsT = p.sb([128, 8, 128], BF16, "wsT", sc)
            p.dma("pool", wsn[:], ws_in.t.rearrange("g p q -> p g q"), reads=[ws_in], writes=[wsn])
            tb = self.tbanks[0]

            def trw(e):
                r_ = None
                for g in range(8):
                    r_ = e.transpose(tb[:, g, :], wsn[:, g, :], self.ident[:, :])
                return r_
            p.op("pe", trw, reads=[wsn, self.ident], writes=[tb])
            p.op("act", lambda e: e.activation(out=wsT[:], in_=tb[:], func=AF.Copy), reads=[tb], writes=[wsT])
            Sg = p.sb([128, 8, 128], F32, "Sg", sc)
            for hh in range(2):
                pb = self.bank(3, 6)
                p.op("pe", lambda e, pb=pb, hh=hh: e.matmul(pb[:, :].rearrange("q (g t) -> q g t", g=4), self.ones[:, :], wsT[:, 4 * hh:4 * hh + 4, :], start=True, stop=True),
                     reads=[self.ones, wsT], writes=[pb])
                p.op("dve", lambda e, pb=pb, hh=hh: e.tensor_copy(out=Sg[:, 4 * hh:4 * hh + 4, :], in_=pb[:, :].rearrange("q (g t) -> q g t", g=4)),
                     reads=[pb], writes=[(Sg, hh)])
            BS = p.sb([128, 8, 128], F32, "BS", sc)
            p.dma("sp", BS[:].rearrange("q g t -> q (g t)"), bs_in.t[0, :].partition_broadcast(128), reads=[bs_in], writes=[BS])
            gv = p.sb([128, 48], F32, "gv", sc)
            bv = p.sb([128, 48], F32, "bv", sc)
            p.dma("sp", gv[:], vg_in.t[0].rearrange("(c q) -> q c", q=128), reads=[vg_in], writes=[gv], allow_slow_non_contiguous=True)
            p.dma("sp", bv[:], vb_in.t[0].rearrange("(c q) -> q c", q=128), reads=[vb_in], writes=[bv], allow_slow_non_contiguous=True)
            for tiles in SUPER:
                ntl = len(tiles)
                ntok = ntl * 128
                g0 = tiles[0] * 128
                blocks = sb_blocks(tiles)
                with contextlib.ExitStack() as st:
                    hT = p.sb([128, DC, SBT], BF16, "hT", st)
                    self.pre(1, tiles, hT)
                    s1 = p.sb([128, 9, 12], F32, "s1", st)
                    s2 = p.sb([128, 9, 12], F32, "s2", st)
                    with contextlib.ExitStack() as sv_:
                        vbs = [p.sb([128, 512], BF16, "vbb", sv_) for _ in range(3)]
                        vj = [p.sb([128, 512], BF16, "vj", sv_) for _ in range(2)]
                        vc = [0]

                        def cbv(lt, nb, pb):
                            v_ = vbs[vc[0] % 3]
                            j_ = vj[vc[0] % 2]
                            vc[0] += 1
                            p.op("act", lambda e: e.activation(out=v_[:], in_=pb[:, 0:512], func=AF.Gelu_apprx_tanh, accum_out=s1[:, lt, nb:nb + 1]),
                                 reads=[pb], writes=[v_, (s1, (lt, nb))])
                            p.op("dve", lambda e: e.scalar_tensor_tensor(out=j_[:], in0=v_[:], scalar=1.0, in1=v_[:], op0=ALU.mult, op1=ALU.mult,
                                                                          accum_out=s2[:, lt, nb:nb + 1]), reads=[v_], writes=[j_, (s2, (lt, nb))])
                            r0 = tiles[lt] * 128
                            p.dma("sp", self.v_scr.t[r0:r0 + 128, nb * 512:(nb + 1) * 512], v_[:], reads=[v_], writes=[(self.v_scr, (tiles[lt], nb))])
                        self.proj_tm(hT, DC, Wi, 6144, 6144, 512, ntl, cbv, sv_)
                    sm = p.sb([128, 9, 8], F32, "sm", st)
                    p.op("dve", lambda e: e.tensor_reduce(out=sm[:, 0:ntl, 0], in_=s1[:, 0:ntl, :], axis=mybir.AxisListType.X, op=ALU.add), reads=[s1], writes=[(sm, 0)])
                    p.op("dve", lambda e: e.tensor_reduce(out=sm[:, 0:ntl, 1], in_=s2[:, 0:ntl, :], axis=mybir.AxisListType.X, op=ALU.add), reads=[s2], writes=[(sm, 1)])
                    p.op("dve", lambda e: e.tensor_scalar(out=sm[:, 0:ntl, 2], in0=sm[:, 0:ntl, 0], scalar1=1.0 / 6144, scalar2=None, op0=ALU.mult), reads=[(sm, 0)], writes=[(sm, 2)])
                    p.op("dve", lambda e: e.tensor_tensor(out=sm[:, 0:ntl, 3], in0=sm[:, 0:ntl, 2], in1=sm[:, 0:ntl, 2], op=ALU.mult), reads=[(sm, 2)], writes=[(sm, 3)])
                    p.op("dve", lambda e: e.scalar_tensor_tensor(out=sm[:, 0:ntl, 4], in0=sm[:, 0:ntl, 1], scalar=1.0 / 6144, in1=sm[:, 0:ntl, 3], op0=ALU.mult, op1=ALU.subtract),
                         reads=[(sm, 1), (sm, 3)], writes=[(sm, 4)])
                    p.op("act", lambda e: e.activation(out=sm[:, 0:ntl, 5], in_=sm[:, 0:ntl, 4], func=AF.Sqrt, scale=1.0, bias=self.epsb[:, :]), reads=[(sm, 4), self.epsb], writes=[(sm, 5)])
                    p.op("dve", lambda e: e.reciprocal(out=sm[:, 0:ntl, 6], in_=sm[:, 0:ntl, 5]), reads=[(sm, 5)], writes=[(sm, 6)])
                    p.op("dve", lambda e: e.tensor_tensor(out=sm[:, 0:ntl, 7], in0=sm[:, 0:ntl, 2], in1=sm[:, 0:ntl, 6], op=ALU.mult), reads=[(sm, 2), (sm, 6)], writes=[(sm, 7)])
                    wsr = p.sb([128, 9, 8, 128], BF16, "wsr", st)
                    amat = p.sb([128, 9, 128], BF16, "amat", st)
                    Tsb = p.sb([128, 9, 8, 128], F32, "Tsb", st)
                    for lt in range(ntl):
                        p.op("dve", lambda e, lt=lt: e.tensor_scalar(out=wsr[:, lt].rearrange("q g t -> q (g t)"), in0=wsT[:].rearrange("q g t -> q (g t)"),
                                                                      scalar1=sm[:, lt, 6:7], scalar2=None, op0=ALU.mult), reads=[wsT, (sm, 6)], writes=[(wsr, lt)])
                        p.op("dve", lambda e, lt=lt: e.tensor_scalar(out=amat[:, lt, :], in0=self.ones[:, :], scalar1=sm[:, lt, 7:8], scalar2=None, op0=ALU.mult),
                             reads=[self.ones, (sm, 7)], writes=[(amat, lt)])
                        for hh in range(2):
                            pb = self.bank(3, 6)
                            p.op("pe", lambda e, pb=pb, lt=lt, hh=hh: e.matmul(pb[:, :].rearrange("q (g t) -> q g t", g=4), amat[:, lt, :], wsT[:, 4 * hh:4 * hh + 4, :], start=True, stop=True),
                                 reads=[(amat, lt), wsT], writes=[pb])
                            p.op("act", lambda e, pb=pb, lt=lt, hh=hh: e.activation(out=Tsb[:, lt, 4 * hh:4 * hh + 4, :], in_=pb[:, :].rearrange("q (g t) -> q g t", g=4), func=AF.Copy),
                                 reads=[pb], writes=[(Tsb, (lt, hh))])
                    vgs = [p.sb([128, 9, 768], BF16, "vg", st) for _ in range(2)]
                    uTs = [p.sb([128, SBT], F32, "uT", st) for _ in range(2)]
                    zcs = [p.sb([128, SBT], BF16, "zc", st) for _ in range(2)]
                    Bcs = [p.sb([128, 128], F32, "Bc", st) for _ in range(2)]
                    d1s = [p.sb([128, 128], F32, "d1", st) for _ in range(3)]
                    nd = [0]

                    def cbu(m, bi, blk, pb):
                        t0, nt, _ = blk
                        g = m // 6
                        vg = vgs[g % 2]
                        uT = uTs[m % 2]
                        if m % 6 == 0 and bi == 0:
                            for lt in range(ntl):
                                r0 = tiles[lt] * 128
                                p.dma("sp", vg[:, lt, :], self.v_scr.t[r0:r0 + 128, g * 768:(g + 1) * 768], reads=[self.v_scr], writes=[(vg, lt)])
                        p.op("act", lambda e: e.activation(out=uT[:, t0:t0 + nt], in_=pb[:, 0:nt], func=AF.Gelu_apprx_tanh), reads=[pb], writes=[(uT, bi)])
                        if bi != len(blocks) - 1:
                            return
                        zc = zcs[m % 2]
                        Bc = Bcs[m % 2]
                        p.op("dve", lambda e: e.scalar_tensor_tensor(out=Bc[:], in0=Sg[:, g, :], scalar=bv[:, m:m + 1], in1=BS[:, g, :], op0=ALU.mult, op1=ALU.add),
                             reads=[Sg, bv, BS], writes=[Bc])
                        for lt in range(ntl):
                            pm = self.bank(0, 3)
                            p.op("pe", lambda e, pm=pm, lt=lt: e.matmul(pm[:, 0:128], vg[:, lt, (m % 6) * 128:(m % 6 + 1) * 128], wsr[:, lt, g, :], start=True, stop=True),
                                 reads=[(vg, lt), (wsr, lt)], writes=[pm])
                            d1 = d1s[nd[0] % 3]
                            nd[0] += 1
                            p.op("dve", lambda e, pm=pm, lt=lt, d1=d1: e.tensor_tensor(out=d1[:], in0=pm[:, 0:128], in1=Tsb[:, lt, g, :], op=ALU.subtract),
                                 reads=[pm, (Tsb, (lt, g // 4))], writes=[d1])
                            p.op("dve", lambda e, d1=d1: e.scalar_tensor_tensor(out=d1[:], in0=d1[:], scalar=gv[:, m:m + 1], in1=Bc[:], op0=ALU.mult, op1=ALU.add),
                                 reads=[d1, gv, Bc], writes=[d1])
                            p.op("dve", lambda e, d1=d1, lt=lt: e.tensor_tensor(out=zc[:, lt * 128:(lt + 1) * 128], in0=uT[:, lt * 128:(lt + 1) * 128], in1=d1[:], op=ALU.mult),
                                 reads=[d1, uT], writes=[(zc, lt)])
                        p.dma("sp", self.oT_scr.t[m, :, g0:g0 + ntok], zc[:, 0:ntok], reads=[zc], writes=[(self.oT_scr, (m, g0))])
                    self.proj_fm(hT, DC, Wi, [m * 128 for m in range(48)], blocks, cbu, st)
        self.out_stage(48, Wo, 256)
        self.post(layer, 1, 1.0, False)

    def build(self):
        p = self.p
        n_sub = 0
        done = False
        for layer in range(DEPTH):
            self.ada(layer)
            for s in range(3):
                if s == 1:
                    kind, j = layer % 3, layer // 3
                    if kind == 0:
                        self.mixer_A(layer, j)
                    elif kind == 1:
                        self.mixer_B(layer, j)
                    else:
                        self.mixer_C(layer, j)
                else:
                    self.ffn(layer, 0 if s == 0 else 1, s)
                n_sub += 1
                if isinstance(self.stop_after, int) and n_sub >= self.stop_after:
                    done = True
                    break
            if done:
                break
        p.marks.append(("final", p.nops))
        p.limit = 0
        with contextlib.ExitStack() as st:
            xt = [p.sb([128, D], F32, "xo", st) for _ in range(3)]
            for ti in range(NT):
                x = xt[ti % 3]
                p.dma("sp", x[:], self.x_cur.t[ti * 128:(ti + 1) * 128, :], reads=[self.x_cur], writes=[x])
                p.dma("sp", self.out.t[ti * 128:(ti + 1) * 128, :], x[:], reads=[x], writes=[(self.out, ti)])
        p.finish()
        return p.nc


def _rmat():
    r = np.zeros((128, 128), np.float32)
    for i in range(64):
        r[2 * i + 1, 2 * i] = -1.0
        r[2 * i, 2 * i + 1] = 1.0
    return r


_ROPE = []


def _rope_tables():
    if not _ROPE:
        rows = SEQ // 64
        row = np.repeat(np.arange(rows, dtype=np.float32), 64)
        col = np.tile(np.arange(64, dtype=np.float32), rows)
        inv = (np.float32(10000.0) ** (-np.arange(0, 64, 2, dtype=np.float32) / np.float32(64))).astype(np.float32)
        ang = np.concatenate([row[:, None] * inv, col[:, None] * inv], axis=-1).astype(np.float32)
        cos = np.cos(ang).astype(np.float32)
        sin = np.sin(ang).astype(np.float32)
        _ROPE.append(np.ascontiguousarray(np.repeat(cos, 2, axis=1).T))
        _ROPE.append(np.ascontiguousarray(np.repeat(sin, 2, axis=1).T))
    return _ROPE


_CACHE = {}


def _get_builder(stop_after=None):
    if stop_after not in _CACHE:
        b = Builder(stop_after)
        b.nc = b.build()
        _CACHE[stop_after] = b
    return _CACHE[stop_after]


def make_in_maps(builder, inputs):
    f = lambda a: np.ascontiguousarray(np.asarray(a, dtype=np.float32))
    maps = []
    for b in range(NUSED):
        m = {}
        for name, spec in builder.ext_specs.items():
            kind = spec[0]
            if kind == "x":
                m[name] = f(np.concatenate([inputs["x"][b], inputs["ctx"][b]], axis=0))
            elif kind == "cvec":
                m[name] = f(np.stack([inputs["c"][b], inputs["c_ctx"]], axis=0))
            elif kind == "full":
                m[name] = f(inputs[spec[1]])
            elif kind == "mat":
                m[name] = f(inputs[spec[1]][spec[2]])
            elif kind == "row":
                a = f(inputs[spec[1]][spec[2]])
                m[name] = a.reshape(1, -1) if a.ndim == 1 else a
            elif kind == "rowrep":
                m[name] = f(np.tile(f(inputs[spec[1]][spec[2]]).reshape(1, -1), (128, 1)))
            elif kind == "rowflat":
                m[name] = f(inputs[spec[1]][spec[2]]).reshape(1, -1)
            elif kind == "const":
                m[name] = f(spec[1](b))
            else:
                raise ValueError(kind)
        maps.append(m)
    return maps


def kernel(**inputs):
    bld = _get_builder()
    maps = make_in_maps(bld, inputs)
    res = run_bass_kernel_spmd(bld.nc, maps, core_ids=list(range(NUSED)))
    out = np.zeros((2, SEQ, D), dtype=np.float32)
    for b in range(NUSED):
        out[b] = res.results[b]["out"][:SEQ]
    return out
```
